# Optimizing a Trainium2 kernel written in Bass

```python
import math
import jax, jax.numpy as jnp
from jax import lax
import numpy as np

D_MODEL = 2048
BATCH = 1
SEQ = 16384
DEPTH = 1

D_MIX = D_MODEL
ATTN_WIDTH = D_MIX // 2
CONV_WIDTH = D_MIX - ATTN_WIDTH
HEAD_DIM = 128
N_ATTN_HEADS = ATTN_WIDTH // HEAD_DIM
CONV_KERNEL = 31
Q_BLOCK = 128
RMS_EPS = 1e-6
LN_EPS = 1e-5
IN_SPLITS = [ATTN_WIDTH, 2 * ATTN_WIDTH, 3 * ATTN_WIDTH, 4 * ATTN_WIDTH,
             4 * ATTN_WIDTH + CONV_WIDTH, 4 * ATTN_WIDTH + 2 * CONV_WIDTH]
D_IN = 4 * ATTN_WIDTH + 3 * CONV_WIDTH

kernel_name = "hybrid_stickbreak_conformer_layer"


def _rmsnorm(x, g):
    xf = x.astype(jnp.float32)
    y = xf * lax.rsqrt(jnp.mean(xf * xf, axis=-1, keepdims=True) + RMS_EPS)
    return (y * g.astype(jnp.float32)).astype(x.dtype)


def _layernorm(x, g, b):
    xf = x.astype(jnp.float32)
    mu = jnp.mean(xf, axis=-1, keepdims=True)
    var = jnp.mean(jnp.square(xf - mu), axis=-1, keepdims=True)
    y = (xf - mu) * lax.rsqrt(var + LN_EPS)
    return (y * g.astype(jnp.float32) + b.astype(jnp.float32)).astype(x.dtype)


def _stick_breaking_attention(q, k, v):
    B, S, H, D = q.shape
    n_blk = S // Q_BLOCK
    scale = 1.0 / math.sqrt(D)
    kh = k.transpose(0, 2, 1, 3)
    vh = v.transpose(0, 2, 1, 3)
    qb = q.reshape(B, n_blk, Q_BLOCK, H, D).transpose(1, 0, 3, 2, 4)
    starts = jnp.arange(n_blk, dtype=jnp.int32) * Q_BLOCK
    key_pos = jnp.arange(S, dtype=jnp.int32)

    def one_block(args):
        qblk, t0 = args
        z = jnp.einsum('bhqd,bhkd->bhqk', qblk, kh,
                       preferred_element_type=jnp.float32) * scale
        q_pos = t0 + jnp.arange(Q_BLOCK, dtype=jnp.int32)
        causal = key_pos[None, :] < q_pos[:, None]
        log_1m_beta = jnp.where(causal, -jax.nn.softplus(z), 0.0)
        suffix = lax.cumsum(log_1m_beta, axis=3, reverse=True) - log_1m_beta
        log_a = jax.nn.log_sigmoid(z) + suffix
        a = jnp.where(causal, jnp.exp(log_a), 0.0).astype(vh.dtype)
        o = jnp.einsum('bhqk,bhkd->bhqd', a, vh, preferred_element_type=jnp.float32)
        return o.astype(vh.dtype)

    out = lax.map(one_block, (qb, starts))
    return out.transpose(1, 0, 3, 2, 4).reshape(B, S, H * D)


def _conformer_conv(u, g_glu, conv_w, conv_b, ln_g, ln_b, w_pw2, b_pw2):
    h = u * jax.nn.sigmoid(g_glu)
    C = h.shape[-1]
    h = lax.conv_general_dilated(
        h, conv_w.astype(h.dtype), window_strides=(1,),
        padding=[(CONV_KERNEL - 1, 0)],
        dimension_numbers=('NWC', 'WIO', 'NWC'),
        feature_group_count=C) + conv_b
    h = _layernorm(h, ln_g, ln_b)
    h = jax.nn.silu(h)
    return h @ w_pw2 + b_pw2


def setup_inputs(seed: int = 0) -> dict:
    key = jax.random.key(seed)
    ks = jax.random.split(key, 11)
    x = jax.random.normal(ks[0], (BATCH, SEQ, D_MODEL), jnp.float32)
    g_pre = 1.0 + 0.02 * jax.random.normal(ks[1], (D_MODEL,), jnp.float32)
    w_in = jax.random.normal(ks[2], (D_MODEL, D_IN), jnp.float32) * D_MODEL ** -0.5
    conv_w = jax.random.normal(ks[3], (CONV_KERNEL, 1, CONV_WIDTH), jnp.float32) * CONV_KERNEL ** -0.5
    conv_b = 0.02 * jax.random.normal(ks[4], (CONV_WIDTH,), jnp.float32)
    ln_g = 1.0 + 0.02 * jax.random.normal(ks[5], (CONV_WIDTH,), jnp.float32)
    ln_b = 0.02 * jax.random.normal(ks[6], (CONV_WIDTH,), jnp.float32)
    w_pw2 = jax.random.normal(ks[7], (CONV_WIDTH, CONV_WIDTH), jnp.float32) * CONV_WIDTH ** -0.5
    b_pw2 = 0.02 * jax.random.normal(ks[8], (CONV_WIDTH,), jnp.float32)
    w_out = jax.random.normal(ks[9], (D_MIX, D_MODEL), jnp.float32) * D_MIX ** -0.5
    g_post = 1.0 + 0.02 * jax.random.normal(ks[10], (D_MODEL,), jnp.float32)
    return {"x": x, "g_pre": g_pre, "w_in": w_in, "conv_w": conv_w, "conv_b": conv_b,
            "ln_g": ln_g, "ln_b": ln_b, "w_pw2": w_pw2, "b_pw2": b_pw2,
            "w_out": w_out, "g_post": g_post}


def reference(x, g_pre, w_in, conv_w, conv_b, ln_g, ln_b, w_pw2, b_pw2, w_out, g_post):
    B, S, _ = x.shape
    for _layer in range(DEPTH):
        h = _rmsnorm(x, g_pre)
        proj = h @ w_in
        q, k, v, z_att, u, g_glu, z_conv = jnp.split(proj, IN_SPLITS, axis=-1)
        q = q.reshape(B, S, N_ATTN_HEADS, HEAD_DIM)
        k = k.reshape(B, S, N_ATTN_HEADS, HEAD_DIM)
        v = v.reshape(B, S, N_ATTN_HEADS, HEAD_DIM)
        y_att = _stick_breaking_attention(q, k, v) * jax.nn.silu(z_att)
        y_conv = _conformer_conv(u, g_glu, conv_w, conv_b, ln_g, ln_b,
                                 w_pw2, b_pw2) * jax.nn.silu(z_conv)
        y = jnp.concatenate([y_att, y_conv], axis=-1) @ w_out
        x = x + _rmsnorm(y, g_post)
    return x
```

```python
import numpy as np
import concourse.bass as bass
import concourse.mybir as mybir
from concourse.bass_utils import run_bass_kernel_spmd

F32 = mybir.dt.float32
BF16 = mybir.dt.bfloat16
I32 = mybir.dt.int32
AF = mybir.ActivationFunctionType
ALU = mybir.AluOpType
AX = mybir.AxisListType

NCORES = 8
DM = 2048
KC = DM // 128
CW = 1024
CK = 31
RMS_EPS = 1e-6
LN_EPS = 1e-5
QSCALE = 1.0 / float(np.sqrt(128.0))
NEGBIG = -30000.0
C_ONES, C_NEGONES, C_NEGTRI, C_IDENT, C_NEGBIG, C_INVMASK, C_ZEROS = range(7)
NCONST = 7


class _Op:
    __slots__ = ("eng", "fn", "dma", "inc", "deps", "signal", "cnt", "val", "idx")


class Sched:
    COMPUTE = ("pe", "act", "dve", "pool")

    def __init__(self, nc, sems):
        self.nc = nc
        self.sems = sems
        self.ops = []
        self.lw = {}
        self.rd = {}
        self.cnt = {e: 0 for e in self.COMPUTE}
        self.dman = {}
        self.waited = {e: {} for e in ("pe", "act", "dve", "pool", "sp")}
        self.nops = 0

    def add(self, eng, fn, r=(), w=(), dma=None, inc=16):
        o = _Op()
        o.eng = eng; o.fn = fn; o.dma = dma; o.inc = inc
        o.signal = False; o.cnt = None; o.val = None
        o.idx = self.nops; self.nops += 1
        deps = {}
        for k in r:
            d = self.lw.get(k)
            if d is not None:
                deps[d.idx] = d
        for k in w:
            d = self.lw.get(k)
            if d is not None:
                deps[d.idx] = d
            for d2 in self.rd.get(k, {}).values():
                deps[d2.idx] = d2
        deps.pop(o.idx, None)
        o.deps = [d for d in deps.values()
                  if not (d.eng == "pe" and eng == "pe" and d.dma is None and dma is None)]
        for d in o.deps:
            d.signal = True
        for k in r:
            rk = self.rd.setdefault(k, {})
            rk[eng if dma is None else ("dma", o.idx)] = o
        for k in w:
            self.lw[k] = o
            self.rd[k] = {}
        self.ops.append(o)
        return o

    def flush(self):
        nc = self.nc
        ops = self.ops
        last = {}
        for o in ops:
            if o.dma is None:
                last[o.eng] = o
        for o in last.values():
            o.signal = True
        for o in ops:
            if o.dma is not None:
                v = self.dman.get(o.dma, 0) + o.inc
                self.dman[o.dma] = v
                o.val = v
            elif o.signal:
                self.cnt[o.eng] += 1
                o.cnt = self.cnt[o.eng]
        fin_cnt = dict(self.cnt)
        fin_dma = dict(self.dman)
        sems = self.sems
        waited = self.waited

        def emit(eng_name, e):
            wd = waited[eng_name]
            for o in ops:
                if o.eng != eng_name:
                    continue
                for d in o.deps:
                    if d.dma is not None:
                        s, v = d.dma, d.val
                    else:
                        s, v = d.eng, d.cnt
                    if wd.get(s, 0) >= v:
                        continue
                    e.wait_ge(sems[s], v)
                    wd[s] = v
                ins = o.fn(e)
                if o.dma is not None:
                    ins.then_inc(sems[o.dma], o.inc)
                elif o.signal:
                    ins.then_inc(sems[o.eng], 1)
            for s, v in list(fin_cnt.items()) + list(fin_dma.items()):
                if v > 0 and wd.get(s, 0) < v:
                    e.wait_ge(sems[s], v)
                    wd[s] = v

        with nc.Block() as block:
            @block.tensor
            def _(e):
                emit("pe", e)

            @block.scalar
            def _(e):
                emit("act", e)

            @block.vector
            def _(e):
                emit("dve", e)

            @block.gpsimd
            def _(e):
                emit("pool", e)

            @block.sync
            def _(e):
                emit("sp", e)
        self.ops = []
        self.lw = {}
        self.rd = {}


def _split(lo, hi, step):
    out = []
    while lo < hi:
        out.append((lo, min(hi, lo + step)))
        lo += step
    return out


def _mm(out, lhsT, rhs, start, stop, skip=False):
    if skip:
        return lambda e: e.matmul(out, lhsT=lhsT, rhs=rhs, start=start, stop=stop, skip_group_check=True)
    return lambda e: e.matmul(out, lhsT=lhsT, rhs=rhs, start=start, stop=stop)


def _act(out, in_, func, scale=1.0, bias=0.0):
    return lambda e: e.activation(out=out, in_=in_, func=func, bias=bias, scale=scale)


def _tt(out, a, b, op):
    return lambda e: e.tensor_tensor(out=out, in0=a, in1=b, op=op)


def _ts(out, a, s1, op0, s2=None, op1=None):
    if op1 is None:
        return lambda e: e.tensor_scalar(out=out, in0=a, scalar1=s1, scalar2=None, op0=op0)
    return lambda e: e.tensor_scalar(out=out, in0=a, scalar1=s1, scalar2=s2, op0=op0, op1=op1)


def _stt(out, a, s, b, op0, op1):
    return lambda e: e.scalar_tensor_tensor(out=out, in0=a, scalar=s, in1=b, op0=op0, op1=op1)


def _cp(out, in_):
    return lambda e: e.tensor_copy(out=out, in_=in_)


def _rcp(out, in_):
    return lambda e: e.reciprocal(out=out, in_=in_)


def _dma(out, in_):
    return lambda e: e.dma_start(out=out, in_=in_)


def _memset(ap, v):
    return lambda e: e.memset(ap, v)


def build_program(S, mode="fused"):
    TOK = S // NCORES
    NT = S // 512
    HALF = min(1024, TOK)
    NH = TOK // HALF
    HW = HALF + 32
    NBLK = S // 128

    nc = bass.Bass("TRN2", target_bir_lowering=False)
    din = lambda n, sh, dt=F32: nc.dram_tensor(n, sh, dt, kind="ExternalInput").ap()
    need_ab = mode != "c"
    need_c = mode != "ab"
    cvec = din("cvec", [128, 32])
    cmat = din("cmat", [128, NCONST * 128])
    if need_ab:
        xT = din("xT", [DM, S])
        xTown = din("xTown", [DM, 32 + TOK])
        wh = din("wh", [DM, 512])
        wconv = din("wconv", [8 * 3 * 128, DM])
        wpw2 = din("wpw2", [CW, CW])
        gpre = din("gpre", [128, KC])
        convw = din("convw", [128, 8 * CK])
    if need_c:
        xown = din("xown", [TOK, DM])
        wout = din("wout", [DM, DM])
        gpost = din("gpost", [128, DM])
    if mode == "fused":
        t0in = din("t0", [1, 1], I32)
    if mode == "ab":
        yatt_loc = nc.dram_tensor("yatt_loc", [128, S], BF16, kind="ExternalOutput")
        yconv_d = nc.dram_tensor("yconv_d", [128, 8 * TOK], BF16, kind="ExternalOutput")
        yatt_all = None
        out = None
    elif mode == "c":
        out = nc.dram_tensor("out", [TOK, DM], F32, kind="ExternalOutput").ap()
        yatt_loc = None
        yatt_all = nc.dram_tensor("yatt_in", [128, 8 * TOK], BF16, kind="ExternalInput")
        yconv_d = nc.dram_tensor("yconv_d", [128, 8 * TOK], BF16, kind="ExternalInput")
    else:
        out = nc.dram_tensor("out", [TOK, DM], F32, kind="ExternalOutput").ap()
        yatt_loc = nc.dram_tensor("yatt_loc", [128, S], BF16)
        yatt_all = nc.dram_tensor("yatt_all", [128 * NCORES, S], BF16)
        yconv_d = nc.dram_tensor("yconv_d", [128, 8 * TOK], BF16)

    if need_ab:
        xT_v = xT.rearrange("(kc p) t -> p kc t", p=128)
        xTown_v = xTown.rearrange("(kc p) t -> p kc t", p=128)
    yconv_v = yconv_d.ap().rearrange("p (c t) -> p c t", c=8)
    if mode == "c":
        yall_v = yatt_all.ap().rearrange("p (r t) -> p r t", r=8)
    elif mode == "fused":
        yall_v = yatt_all.ap().rearrange("(r p) t -> p r t", p=128)

    import contextlib
    es = contextlib.ExitStack()
    with es:
        def sb(name, shape, dt=F32):
            return es.enter_context(nc.sbuf_tensor(name, shape, dt))

        sem_names = ["pe", "act", "dve", "pool", "init", "xs0", "xs1", "xs2", "ws0", "ws1",
                     "yo0", "yo1", "ycd", "cc", "ya0", "ya1", "yc0", "yc1", "xr0", "xr1",
                     "o0", "o1"]
        sems = {n: es.enter_context(nc.semaphore("s_" + n)) for n in sem_names}
        PS = [None] * 8
        greg = es.enter_context(nc.gpsimd.register("greg"))
        sch = Sched(nc, sems)
        A = sch.add

        cstage = sb("cstage", [128, NCONST * 128])
        cb = sb("cb", [128, NCONST, 128], BF16)
        gpre_sb = sb("gpre_sb", [128, KC])
        convw_sb = sb("convw_sb", [128, 8, CK])
        cvec_sb = sb("cvec_sb", [128, 4, 8])
        ncvec_sb = sb("ncvec_sb", [128, 4, 8])
        xs = [sb("xs%d" % i, [128, 2, 512]) for i in range(3)]
        sq = [sb("sq%d" % i, [128, 2, 512], BF16) for i in range(2)]
        TT = {}

        def tmp(name, dt=F32, n=512):
            if name not in TT:
                TT[name] = sb("t_" + name, [128, n], dt)
            return TT[name]

        ones_bf = cb[:, C_ONES, :]
        negones_bf = cb[:, C_NEGONES, :]
        negtri_bf = cb[:, C_NEGTRI, :]
        ident_bf = cb[:, C_IDENT, :]
        negbig_bf = cb[:, C_NEGBIG, :]
        invmask_bf = cb[:, C_INVMASK, :]
        zeros_bf = cb[:, C_ZEROS, :]
        ones_f = cstage[:, C_ONES * 128:(C_ONES + 1) * 128]

        A("sp", _dma(cstage[:], cmat[:, :]), w=["cstage"], dma="init")
        if need_ab:
            A("sp", _dma(gpre_sb[:], gpre[:, :]), w=["gpre"], dma="init")
            A("sp", _dma(convw_sb[:], convw.rearrange("p (c k) -> p c k", c=8)), w=["convw"], dma="init")
        A("sp", _dma(cvec_sb[:], cvec.rearrange("p (a c) -> p a c", a=4)), w=["cvec"], dma="init")
        sch.flush()
        A("dve", _cp(cb[:].rearrange("p a b -> p (a b)"), cstage[:]), w=["cb"])
        A("dve", _ts(ncvec_sb[:].rearrange("p a b -> p (a b)"),
                     cvec_sb[:].rearrange("p a b -> p (a b)"), -1.0, ALU.mult), w=["ncvec"])

        xs_ctr = [0]
        sq_ctr = [0]

        def load_x_piece(src_ap, w, xb_dst_fn, ps_ap, first, last, pc):
            sl = xs_ctr[0] % 3
            xs_ctr[0] += 1
            ssl = sq_ctr[0] % 2
            sq_ctr[0] += 1
            A("sp", _dma(xs[sl][:, :, 0:w], src_ap), w=[("xs", sl)], dma="xs%d" % sl)
            A("dve", _tt(sq[ssl][:, :, 0:w], xs[sl][:, :, 0:w], xs[sl][:, :, 0:w], ALU.mult),
              r=[("xs", sl)], w=[("sq", ssl)])
            for kk in range(2):
                kc = 2 * pc + kk
                A("dve", _ts(xb_dst_fn(kc), xs[sl][:, kk, 0:w], gpre_sb[:, kc:kc + 1], ALU.mult),
                  r=[("xs", sl), "gpre"], w=[("xb", kc)])
            for kk in range(2):
                A("pe", _mm(ps_ap, ones_bf, sq[ssl][:, kk, 0:w],
                            first and kk == 0, last and kk == 1),
                  r=[("sq", ssl), "cb"], w=[("ps", 7)])

        esA = contextlib.ExitStack()
        for _ph in ([0] if mode != "c" else []):
            def sbA(name, shape, dt=F32):
                return esA.enter_context(nc.sbuf_tensor(name, shape, dt))

            for _i in range(8):
                PS[_i] = esA.enter_context(nc.psum_tensor("psA%d" % _i, [128, 512], F32))
            xbA = sbA("xbA", [128, KC, HW], BF16)
            rstdA = sbA("rstdA", [128, HW])
            cA = sbA("cA", [128, 8, HALF])
            hA = [sbA("hA%d" % i, [128, HW], BF16) for i in range(2)]
            Dk = sbA("Dk", [128, CK, 128], BF16)
            wst = [sbA("wst%d" % i, [128, KC * 128]) for i in range(2)]
            wcb = [sbA("wcb%d" % i, [128, 3, KC * 128], BF16) for i in range(2)]
            pw2b = sbA("pw2b", [128, 8, CW], BF16)
            ycs = sbA("ycs", [128, 8, HALF], BF16)
            hs = sbA("hs", [128, 8, 512], BF16)
            tA = {n: sbA("tA_" + n, [128, 512]) for n in
                  ("ln", "t1", "t2", "t3", "t4", "t5", "tm", "tq", "trs", "csq0", "csq1")}
            ws_ctr = [0]
            tctr = [0]
            cctr = [0]

            def load_w(src_ap, dst_ap, eng):
                sl = ws_ctr[0] % 2
                ws_ctr[0] += 1
                A("sp", _dma(wst[sl][:], src_ap), w=[("wst", sl)], dma="ws%d" % sl)
                A(eng, _cp(dst_ap, wst[sl][:]), r=[("wst", sl)], w=["wdst"])

            wpw2_v = wpw2.rearrange("(ci p) co -> p ci co", p=128)
            for ci2 in range(4):
                sl = ws_ctr[0] % 2
                ws_ctr[0] += 1
                A("sp", _dma(wst[sl][:].rearrange("p (a b) -> p a b", a=2),
                             wpw2_v[:, 2 * ci2:2 * ci2 + 2, :]), w=[("wst", sl)], dma="ws%d" % sl)
                A("dve", _cp(pw2b[:, 2 * ci2:2 * ci2 + 2, :],
                             wst[sl][:].rearrange("p (a b) -> p a b", a=2)),
                  r=[("wst", sl)], w=["pw2b"])

            for hf in range(NH):
                base = hf * HALF
                tiles = _split(0, HW, 512)
                for (lo, hi) in tiles:
                    w = hi - lo
                    for pc in range(8):
                        load_x_piece(xTown_v[:, 2 * pc:2 * pc + 2, base + lo:base + hi], w,
                                     lambda kc, lo=lo, hi=hi: xbA[:, kc, lo:hi],
                                     PS[7][:, 0:w], pc == 0, pc == 7, pc)
                    A("act", _act(tA["ln"][:, 0:w], PS[7][:, 0:w], AF.Ln, scale=1.0 / DM, bias=RMS_EPS),
                      r=[("ps", 7)], w=["tA_ln"])
                    A("act", _act(rstdA[:, lo:hi], tA["ln"][:, 0:w], AF.Exp, scale=-0.5),
                      r=["tA_ln"], w=["rstdA"])
                for i in range(8):
                    wsl = i % 2
                    for kind in range(3):
                        row0 = (i * 3 + kind) * 128
                        sl = ws_ctr[0] % 2
                        ws_ctr[0] += 1
                        A("sp", _dma(wst[sl][:], wconv[row0:row0 + 128, :]), w=[("wst", sl)],
                          dma="ws%d" % sl)
                        A("dve" if kind != 1 else "pool", _cp(wcb[wsl][:, kind, :], wst[sl][:]),
                          r=[("wst", sl)], w=[("wcb", wsl, kind)])
                    hsl = i % 2
                    for (lo, hi) in tiles:
                        w = hi - lo
                        bset = 3 * (tctr[0] % 2)
                        tctr[0] += 1
                        for kind in range(3):
                            for kc in range(KC):
                                A("pe", _mm(PS[bset + kind][:, 0:w], wcb[wsl][:, kind, kc * 128:(kc + 1) * 128],
                                            xbA[:, kc, lo:hi], kc == 0, kc == KC - 1),
                                  r=[("wcb", wsl, kind), ("xb", kc)], w=[("ps", bset + kind)])
                        A("dve", _tt(tA["t1"][:, 0:w], PS[bset + 1][:, 0:w], rstdA[:, lo:hi], ALU.mult),
                          r=[("ps", bset + 1), "rstdA"], w=["t1"])
                        A("act", _act(tA["t2"][:, 0:w], tA["t1"][:, 0:w], AF.Exp, scale=-1.0),
                          r=["t1"], w=["t2"])
                        A("act", _act(tA["t2"][:, 0:w], tA["t2"][:, 0:w], AF.Ln, bias=1.0), r=["t2"], w=["t2"])
                        A("act", _act(tA["t2"][:, 0:w], tA["t2"][:, 0:w], AF.Exp, scale=-1.0), r=["t2"], w=["t2"])
                        A("dve", _tt(tA["t3"][:, 0:w], PS[bset][:, 0:w], rstdA[:, lo:hi], ALU.mult),
                          r=[("ps", bset), "rstdA"], w=["t3"])
                        A("pool", _tt(hA[hsl][:, lo:hi], tA["t3"][:, 0:w], tA["t2"][:, 0:w], ALU.mult),
                          r=["t3", "t2"], w=[("hA", hsl)])
                        lo2 = max(lo, 32)
                        if lo2 < hi:
                            w2 = hi - lo2
                            o2 = lo2 - lo
                            A("dve", _tt(tA["t4"][:, 0:w2], PS[bset + 2][:, o2:w], rstdA[:, lo2:hi], ALU.mult),
                              r=[("ps", bset + 2), "rstdA"], w=["t4"])
                            A("act", _act(tA["t5"][:, 0:w2], tA["t4"][:, 0:w2], AF.Exp, scale=-1.0),
                              r=["t4"], w=["t5"])
                            A("act", _act(tA["t5"][:, 0:w2], tA["t5"][:, 0:w2], AF.Ln, bias=1.0),
                              r=["t5"], w=["t5"])
                            A("act", _act(tA["t5"][:, 0:w2], tA["t5"][:, 0:w2], AF.Exp, scale=-1.0),
                              r=["t5"], w=["t5"])
                            A("pool", _tt(ycs[:, i, lo2 - 32:hi - 32], tA["t4"][:, 0:w2],
                                          tA["t5"][:, 0:w2], ALU.mult),
                              r=["t4", "t5"], w=[("ycs", i)])
                    for k in range(CK):
                        A("dve", _ts(Dk[:, k, :], cstage[:, C_IDENT * 128:(C_IDENT + 1) * 128],
                                     convw_sb[:, i, k:k + 1], ALU.mult),
                          r=["cstage", "convw"], w=["Dk"])
                    for (lo, hi) in _split(0, HALF, 512):
                        w = hi - lo
                        cb_ = 6 + cctr[0] % 2
                        cctr[0] += 1
                        for k in range(CK):
                            A("pe", _mm(PS[cb_][:, 0:w], Dk[:, k, :], hA[hsl][:, lo + k + 2:hi + k + 2],
                                        k == 0, k == CK - 1),
                              r=["Dk", ("hA", hsl)], w=[("ps", cb_)])
                        A("act", _act(cA[:, i, lo:hi], PS[cb_][:, 0:w], AF.Identity,
                                      bias=cvec_sb[:, 0, i:i + 1]),
                          r=[("ps", cb_), "cvec"], w=[("cA", i)])
                for (lo, hi) in _split(0, HALF, 512):
                    w = hi - lo
                    for i in range(8):
                        A("pe", _mm(PS[4][:, 0:w], ones_f, cA[:, i, lo:hi], i == 0, i == 7),
                          r=[("cA", i), "cstage"], w=[("ps", 4)])
                    for i in range(8):
                        cs = "csq%d" % (i % 2)
                        A("act", _act(tA[cs][:, 0:w], cA[:, i, lo:hi], AF.Square), r=[("cA", i)], w=[cs])
                        A("pe", _mm(PS[5][:, 0:w], ones_f, tA[cs][:, 0:w], i == 0, i == 7),
                          r=[cs, "cstage"], w=[("ps", 5)])
                    A("dve", _ts(tA["tm"][:, 0:w], PS[4][:, 0:w], 1.0 / CW, ALU.mult), r=[("ps", 4)], w=["tm"])
                    A("dve", _tt(tA["tq"][:, 0:w], tA["tm"][:, 0:w], tA["tm"][:, 0:w], ALU.mult),
                      r=["tm"], w=["tq"])
                    A("dve", _stt(tA["tq"][:, 0:w], PS[5][:, 0:w], 1.0 / CW, tA["tq"][:, 0:w],
                                  ALU.mult, ALU.subtract), r=[("ps", 5), "tq"], w=["tq"])
                    A("act", _act(tA["tq"][:, 0:w], tA["tq"][:, 0:w], AF.Ln, bias=LN_EPS), r=["tq"], w=["tq"])
                    A("act", _act(tA["trs"][:, 0:w], tA["tq"][:, 0:w], AF.Exp, scale=-0.5), r=["tq"], w=["trs"])
                    for i in range(8):
                        A("dve", _tt(tA["t1"][:, 0:w], cA[:, i, lo:hi], tA["tm"][:, 0:w], ALU.subtract),
                          r=[("cA", i), "tm"], w=["t1"])
                        A("pool", _tt(tA["t1"][:, 0:w], tA["t1"][:, 0:w], tA["trs"][:, 0:w], ALU.mult),
                          r=["t1", "trs"], w=["t1"])
                        A("act", _act(tA["t2"][:, 0:w], tA["t1"][:, 0:w], AF.Exp,
                                      scale=ncvec_sb[:, 1, i:i + 1], bias=ncvec_sb[:, 2, i:i + 1]),
                          r=["t1", "ncvec"], w=["t2"])
                        A("dve", _ts(tA["t3"][:, 0:w], tA["t1"][:, 0:w], cvec_sb[:, 1, i:i + 1], ALU.mult,
                                     cvec_sb[:, 2, i:i + 1], ALU.add), r=["t1", "cvec"], w=["t3"])
                        A("act", _act(tA["t2"][:, 0:w], tA["t2"][:, 0:w], AF.Ln, bias=1.0), r=["t2"], w=["t2"])
                        A("act", _act(tA["t2"][:, 0:w], tA["t2"][:, 0:w], AF.Exp, scale=-1.0), r=["t2"], w=["t2"])
                        A("pool", _tt(hs[:, i, 0:w], tA["t3"][:, 0:w], tA["t2"][:, 0:w], ALU.mult),
                          r=["t3", "t2"], w=[("hs", i)])
                    for co in range(8):
                        bank = 2 + co % 2
                        for ci in range(8):
                            A("pe", _mm(PS[bank][:, 0:w], pw2b[:, ci, co * 128:(co + 1) * 128],
                                        hs[:, ci, 0:w], ci == 0, ci == 7),
                              r=["pw2b", ("hs", ci)], w=[("ps", bank)])
                        A("dve", _stt(ycs[:, co, lo:hi], PS[bank][:, 0:w], cvec_sb[:, 3, co:co + 1],
                                      ycs[:, co, lo:hi], ALU.add, ALU.mult),
                          r=[("ps", bank), ("ycs", co), "cvec"], w=[("ycs", co)])
                A("sp", _dma(yconv_v[:, :, base:base + HALF], ycs[:]),
                  r=[("ycs", i) for i in range(8)], w=["yconv_d"], dma="ycd")
            sch.flush()
            esA.close()

        esB = contextlib.ExitStack()
        for _ph in ([0] if mode != "c" else []):
            def sbB(name, shape, dt=F32):
                return esB.enter_context(nc.sbuf_tensor(name, shape, dt))

            ZP = [esB.enter_context(nc.psum_tensor("zp%d" % _i, [128, 1024], F32)) for _i in range(3)]
            OB = esB.enter_context(nc.psum_tensor("ob", [128, 512], F32))
            PS[7] = esB.enter_context(nc.psum_tensor("pj", [128, 512], F32))
            kT = sbB("kT", [128, S], BF16)
            vv = sbB("vv", [128, NBLK, 128], BF16)
            whb = sbB("whb", [128, KC, 512], BF16)
            whst = sbB("whst", [128, 4, 512])
            xb = sbB("xb", [128, KC, 512], BF16)
            qT = [sbB("qT%d" % i, [128, 512], BF16) for i in range(2)]
            szt = [sbB("sz%d" % i, [128, 512]) for i in range(2)]
            vTt = sbB("vTt", [128, 512], BF16)
            rstd = sbB("rstd", [128, 512])
            tB = {n: sbB("tB_" + n, [128, 512]) for n in ("ln", "tz", "te")}
            Et = [sbB("E%d" % i, [128, 2, 512]) for i in range(3)]
            spb = [sbB("spb%d" % i, [128, 2, 512], BF16) for i in range(3)]
            Rt = sbB("R", [128, 512])
            Rb = [sbB("Rb%d" % i, [128, 512], BF16) for i in range(2)]
            ab = [sbB("ab%d" % i, [128, 2, 512], BF16) for i in range(2)]
            yo = [sbB("yo%d" % i, [128, 512], BF16) for i in range(2)]

            wh_v = wh.rearrange("(kc p) n -> p kc n", p=128)
            for q4 in range(4):
                A("sp", _dma(whst[:], wh_v[:, 4 * q4:4 * q4 + 4, :]), w=["whst"], dma="ws0")
                A("dve", _cp(whb[:, 4 * q4:4 * q4 + 4, :], whst[:]), r=["whst"], w=["whb"])

            def proj_gen(j):
                t0 = 512 * j
                qs = j % 2
                for pc in range(8):
                    load_x_piece(xT_v[:, 2 * pc:2 * pc + 2, t0:t0 + 512], 512,
                                 lambda kc: xb[:, kc, :], PS[7][:, :], pc == 0, pc == 7, pc)
                    yield
                A("act", _act(tB["ln"][:], PS[7][:], AF.Ln, scale=1.0 / DM, bias=RMS_EPS),
                  r=[("ps", 7)], w=["tB_ln"])
                A("act", _act(rstd[:], tB["ln"][:], AF.Exp, scale=-0.5), r=["tB_ln"], w=["rstd"])
                yield

                def chain(which):
                    for kc in range(KC):
                        A("pe", _mm(PS[7][:, :], whb[:, kc, which * 128:(which + 1) * 128], xb[:, kc, :],
                                    kc == 0, kc == KC - 1),
                          r=["whb", ("xb", kc)], w=[("ps", 7)])
                        if kc % 4 == 3 and kc != KC - 1:
                            yield
                yield from chain(0)
                A("dve", _stt(qT[qs][:], PS[7][:], QSCALE, rstd[:], ALU.mult, ALU.mult),
                  r=[("ps", 7), "rstd"], w=[("qT", qs)])
                yield
                yield from chain(1)
                A("dve", _tt(kT[:, t0:t0 + 512], PS[7][:], rstd[:], ALU.mult),
                  r=[("ps", 7), "rstd"], w=[("kT", 4 * j + b) for b in range(4)])
                yield
                yield from chain(2)
                A("dve", _tt(vTt[:], PS[7][:], rstd[:], ALU.mult), r=[("ps", 7), "rstd"], w=["vTt"])
                for b in range(4):
                    A("pe", _mm(PS[7][:, b * 128:(b + 1) * 128], vTt[:, b * 128:(b + 1) * 128], ident_bf,
                                True, True), r=["vTt", "cb"], w=[("ps", 7)])
                A("dve", _cp(vv[:, 4 * j:4 * j + 4, :].rearrange("p a b -> p (a b)"), PS[7][:]),
                  r=[("ps", 7)], w=[("v", 4 * j + b) for b in range(4)])
                yield
                yield from chain(3)
                A("dve", _tt(tB["tz"][:], PS[7][:], rstd[:], ALU.mult), r=[("ps", 7), "rstd"], w=["tz"])
                A("act", _act(tB["te"][:], tB["tz"][:], AF.Exp, scale=-1.0), r=["tz"], w=["te"])
                A("dve", _ts(tB["te"][:], tB["te"][:], 1.0, ALU.add), r=["te"], w=["te"])
                A("dve", _rcp(tB["te"][:], tB["te"][:]), r=["te"], w=["te"])
                A("pool", _tt(szt[qs][:], tB["tz"][:], tB["te"][:], ALU.mult), r=["tz", "te"], w=[("sz", qs)])
                yield

            units = []
            for j in range(NT):
                lst = [[(4 * j + r, 128 * r)] for r in (3, 2, 1, 0)]
                kbs = list(range(4 * j - 1, -1, -1))
                for p in range(0, len(kbs), 2):
                    lst.append([(kbs[p], 0), (kbs[p + 1], 0)])
                for idx, subs in enumerate(lst):
                    last = idx == len(lst) - 1
                    units.append(dict(j=j, subs=subs, diag=idx < 4, first=idx == 0, last=last,
                                      c0n=None if last else lst[idx + 1][0][1]))
            G = len(units)

            def U1pe(u):
                un = units[u]
                qs = un["j"] % 2
                zb = u % 3
                for si, (kb, c0) in enumerate(un["subs"]):
                    A("pe", _mm(ZP[zb][:, si * 512 + c0:(si + 1) * 512], kT[:, kb * 128:(kb + 1) * 128],
                                qT[qs][:, c0:512], True, not un["diag"]),
                      r=[("kT", kb), ("qT", qs)], w=[("zp", zb, si)])
                    if un["diag"]:
                        A("pe", _mm(ZP[zb][:, c0:c0 + 128], negbig_bf, invmask_bf, False, True),
                          r=["cb"], w=[("zp", zb, si)])

            def U1act(u):
                un = units[u]
                zb = u % 3
                eb = u % 3
                if len(un["subs"]) == 2:
                    A("act", _act(Et[eb][:].rearrange("p a b -> p (a b)"), ZP[zb][:, :], AF.Exp),
                      r=[("zp", zb, 0), ("zp", zb, 1)], w=[("E", eb)])
                else:
                    c0 = un["subs"][0][1]
                    A("act", _act(Et[eb][:, 0, c0:512], ZP[zb][:, c0:512], AF.Exp),
                      r=[("zp", zb, 0)], w=[("E", eb)])

            def U2act(u):
                un = units[u]
                eb = u % 3
                if len(un["subs"]) == 2:
                    A("act", _act(spb[eb][:].rearrange("p a b -> p (a b)"),
                                  Et[eb][:].rearrange("p a b -> p (a b)"), AF.Ln, bias=1.0),
                      r=[("E", eb)], w=[("sp", eb)])
                else:
                    c0 = un["subs"][0][1]
                    A("act", _act(spb[eb][:, 0, c0:512], Et[eb][:, 0, c0:512], AF.Ln, bias=1.0),
                      r=[("E", eb)], w=[("sp", eb)])

            def U2pe(u):
                un = units[u]
                eb = u % 3
                zb = u % 3
                for si, (kb, c0) in enumerate(un["subs"]):
                    Lr = ZP[zb][:, si * 512 + c0:(si + 1) * 512]
                    ops = [(negtri_bf, spb[eb][:, si, c0:512], ["cb", ("sp", eb)])]
                    if not un["first"]:
                        ops.append((negones_bf, Rb[u % 2][:, c0:512], ["cb", ("Rb", u % 2)]))
                    if si == 1:
                        ops.append((negones_bf, spb[eb][:, 0, :], ["cb", ("sp", eb)]))
                    for oi, (lh, rh, rk) in enumerate(ops):
                        A("pe", _mm(Lr, lh, rh, False, oi == len(ops) - 1, skip=True),
                          r=rk + [("E", eb)], w=[("zp", zb, si)])
                if un["first"]:
                    A("pool", _memset(Rt[:], 0.0), w=["R"])
                if not un["last"]:
                    for si, (kb, c0) in enumerate(un["subs"]):
                        A("pool", _tt(Rt[:, c0:512], Rt[:, c0:512], spb[eb][:, si, c0:512], ALU.add),
                          r=["R", ("sp", eb)], w=["R"])
                    c0n = un["c0n"]
                    A("dve", _cp(Rb[(u + 1) % 2][:, c0n:512], Rt[:, c0n:512]),
                      r=["R"], w=[("Rb", (u + 1) % 2)])

            def U3act(u):
                un = units[u]
                zb = u % 3
                if len(un["subs"]) == 2:
                    A("act", _act(ab[u % 2][:].rearrange("p a b -> p (a b)"), ZP[zb][:, :], AF.Exp),
                      r=[("zp", zb, 0), ("zp", zb, 1)], w=[("a", u % 2)])
                else:
                    c0 = un["subs"][0][1]
                    A("act", _act(ab[u % 2][:, 0, c0:512], ZP[zb][:, c0:512], AF.Exp),
                      r=[("zp", zb, 0)], w=[("a", u % 2)])

            def U3pe(u):
                un = units[u]
                j = un["j"]
                qs = j % 2
                nsub = len(un["subs"])
                if un["first"]:
                    A("pe", _mm(OB[:, :], zeros_bf, qT[qs][:, :], True, False),
                      r=["cb", ("qT", qs)], w=["ob"])
                for si, (kb, c0) in enumerate(un["subs"]):
                    A("pe", _mm(OB[:, c0:512], vv[:, kb, :], ab[u % 2][:, si, c0:512], False,
                                un["last"] and si == nsub - 1),
                      r=[("v", kb), ("a", u % 2)], w=["ob"])
                if un["last"]:
                    A("dve", _tt(yo[qs][:], OB[:, :], szt[qs][:], ALU.mult),
                      r=["ob", ("sz", qs)], w=[("yo", qs)])
                    A("sp", _dma(yatt_loc[:, 512 * j:512 * j + 512], yo[qs][:]),
                      r=[("yo", qs)], w=["yatt_loc"], dma="yo%d" % qs)

            gen = proj_gen(0)
            for _ in gen:
                pass
            gen = proj_gen(1) if NT > 1 else iter(())
            gen_j = 1

            def ensure_proj(i):
                nonlocal gen, gen_j
                if i < G and units[i]["first"] and units[i]["j"] == gen_j:
                    for _ in gen:
                        pass
                    gen_j += 1
                    gen = proj_gen(gen_j) if gen_j < NT else iter(())

            U1pe(0)
            for it in range(G + 2):
                if 0 <= it - 1 < G:
                    U2act(it - 1)
                if it < G:
                    U1act(it)
                if 0 <= it - 2 < G:
                    U3act(it - 2)
                if 0 <= it - 1 < G:
                    U2pe(it - 1)
                if 0 <= it - 2 < G:
                    U3pe(it - 2)
                if it + 1 < G:
                    ensure_proj(it + 1)
                    U1pe(it + 1)
                next(gen, None)
            sch.flush()
            esB.close()

        esC = contextlib.ExitStack()
        for _ph in ([0] if mode != "ab" else []):
            def sbC(name, shape, dt=F32):
                return esC.enter_context(nc.sbuf_tensor(name, shape, dt))

            for _i in range(8):
                PS[_i] = esC.enter_context(nc.psum_tensor("psC%d" % _i, [128, 512], F32))
            woutb = sbC("woutb", [128, KC, DM], BF16)
            wst2 = [sbC("wst2_%d" % i, [128, DM]) for i in range(2)]
            ycat = [sbC("ycat%d" % i, [128, KC, 512], BF16) for i in range(2)]
            xres = [sbC("xres%d" % i, [128, DM]) for i in range(2)]
            ot = [sbC("ot%d" % i, [128, DM]) for i in range(2)]
            gpost_sb = sbC("gpost_sb", [128, DM])
            junk = sbC("junk", [128, DM])
            ss1 = sbC("ss1", [128, 1])
            rs1 = sbC("rs1", [128, 1])

            def coll(e):
                return e.collective_compute("AllGather", ALU.bypass,
                                            replica_groups=[list(range(NCORES))],
                                            ins=[yatt_loc.ap().opt()], outs=[yatt_all.ap().opt()])
            if mode == "fused":
                A("pool", coll, r=["yatt_loc"], w=["yatt_all"], dma="cc", inc=1)
            A("sp", _dma(gpost_sb[:], gpost[:, :]), w=["gpost"], dma="init")
            for kc in range(KC):
                sl = kc % 2
                A("sp", _dma(wst2[sl][:], wout[kc * 128:(kc + 1) * 128, :]), w=[("wst2", sl)],
                  dma="ws%d" % sl)
                A("dve" if kc % 2 == 0 else "pool", _cp(woutb[:, kc, :], wst2[sl][:]),
                  r=[("wst2", sl)], w=["woutb"])

            state = {"val": None}

            def load_ya(dst, tb4):
                def fn(e):
                    if mode == "c":
                        return e.dma_start(out=dst, in_=yall_v[:, :, tb4 * 512:tb4 * 512 + 512])
                    if state["val"] is None:
                        e.reg_load(greg, t0in[0:1, 0:1])
                        state["val"] = e.snap(greg)
                    src = yall_v[:, :, tb4 * 512:]
                    return e.dma_start(out=dst, in_=src[:, :, bass.ds(state["val"], 512)])
                return fn

            NTB = TOK // 128
            for tb in range(NTB):
                tb4, within = tb // 4, tb % 4
                ysl = tb4 % 2
                if within == 0:
                    A("pool", load_ya(ycat[ysl][:, 0:8, :], tb4), r=["yatt_all"], w=[("ycat", ysl, 0)],
                      dma="ya%d" % ysl)
                    A("sp", _dma(ycat[ysl][:, 8:16, :], yconv_v[:, :, tb4 * 512:tb4 * 512 + 512]),
                      w=[("ycat", ysl, 1)], dma="yc%d" % ysl)
                xsl = tb % 2
                A("sp", _dma(xres[xsl][:], xown[tb * 128:(tb + 1) * 128, :]), w=[("xres", xsl)],
                  dma="xr%d" % xsl)
                banks = [0, 1, 2, 3] if tb % 2 == 0 else [4, 5, 6, 7]
                for cg in range(4):
                    for kc in range(KC):
                        A("pe", _mm(PS[banks[cg]][:, :], ycat[ysl][:, kc, within * 128:(within + 1) * 128],
                                    woutb[:, kc, cg * 512:(cg + 1) * 512], kc == 0, kc == KC - 1),
                          r=[("ycat", ysl, kc // 8), "woutb"], w=[("ps", banks[cg])])
                for cg in range(4):
                    A("act", _act(junk[:, cg * 512:(cg + 1) * 512], PS[banks[cg]][:, :], AF.Square),
                      r=[("ps", banks[cg])], w=["junk"])
                A("dve", lambda e: e.reduce_sum(out=ss1[:], in_=junk[:], axis=AX.X), r=["junk"], w=["ss1"])
                A("act", _act(ss1[:], ss1[:], AF.Ln, scale=1.0 / DM, bias=RMS_EPS), r=["ss1"], w=["ss1"])
                A("act", _act(rs1[:], ss1[:], AF.Exp, scale=-0.5), r=["ss1"], w=["rs1"])
                osl = tb % 2
                for cg in range(4):
                    cs = slice(cg * 512, (cg + 1) * 512)
                    A("dve", _stt(ot[osl][:, cs], PS[banks[cg]][:, :], rs1[:, 0:1], gpost_sb[:, cs],
                                  ALU.mult, ALU.mult),
                      r=[("ps", banks[cg]), "rs1", "gpost"], w=[("ot", osl, cg)])
                    A("pool", _tt(ot[osl][:, cs], ot[osl][:, cs], xres[xsl][:, cs], ALU.add),
                      r=[("ot", osl, cg), ("xres", xsl)], w=[("ot", osl, cg)])
                A("sp", _dma(out[tb * 128:(tb + 1) * 128, :], ot[osl][:]),
                  r=[("ot", osl, cg) for cg in range(4)], w=["out"], dma="o%d" % osl)
            sch.flush()
            esC.close()
    return nc


def _host_consts():
    p = np.arange(128)[:, None]
    c = np.arange(128)[None, :]
    m = np.zeros((128, NCONST, 128), np.float32)
    m[:, C_ONES] = 1.0
    m[:, C_NEGONES] = -1.0
    m[:, C_NEGTRI] = -(p >= c).astype(np.float32)
    m[:, C_IDENT] = (p == c).astype(np.float32)
    m[:, C_NEGBIG] = NEGBIG * (p == c).astype(np.float32)
    m[:, C_INVMASK] = (p >= c).astype(np.float32)
    return np.ascontiguousarray(m.reshape(128, NCONST * 128))


def make_in_maps(x, g_pre, w_in, conv_w, conv_b, ln_g, ln_b, w_pw2, b_pw2, w_out, g_post):
    f = lambda a: np.ascontiguousarray(np.asarray(a, dtype=np.float32))
    x = f(x)[0]
    S = x.shape[0]
    TOK = S // NCORES
    xT = np.ascontiguousarray(x.T)
    w_in = f(w_in)
    AW = 1024
    vec8 = lambda v: f(v).reshape(8, 128).T
    cvec = np.ascontiguousarray(np.stack([vec8(conv_b), vec8(ln_g), vec8(ln_b), vec8(b_pw2)], axis=1)
                                .reshape(128, 32))
    convw = np.ascontiguousarray(f(conv_w)[:, 0, :].T.reshape(8, 128, CK).transpose(1, 0, 2)
                                 .reshape(128, 8 * CK))
    gpre = np.ascontiguousarray(f(g_pre).reshape(KC, 128).T)
    gpost = np.ascontiguousarray(np.broadcast_to(f(g_post)[None, :], (128, DM)))
    wc = np.stack([w_in[:, 4 * AW:5 * AW], w_in[:, 5 * AW:6 * AW], w_in[:, 6 * AW:7 * AW]], axis=0)
    wc = wc.reshape(3, KC, 128, 8, 128).transpose(3, 0, 2, 1, 4)
    wconv = np.ascontiguousarray(wc.reshape(8 * 3 * 128, KC * 128))
    cmat = _host_consts()
    maps = []
    for c in range(NCORES):
        T0 = c * TOK
        xTown = np.zeros((DM, 32 + TOK), np.float32)
        xTown[:, 32:] = xT[:, T0:T0 + TOK]
        if c > 0:
            xTown[:, :32] = xT[:, T0 - 32:T0]
        whc = np.concatenate([w_in[:, k * AW + c * 128:k * AW + (c + 1) * 128] for k in range(4)], axis=1)
        maps.append({
            "xT": xT, "xTown": xTown, "xown": np.ascontiguousarray(x[T0:T0 + TOK]),
            "wh": np.ascontiguousarray(whc), "wconv": wconv, "wpw2": f(w_pw2), "wout": f(w_out),
            "gpre": gpre, "convw": convw, "cvec": cvec, "gpost": gpost, "cmat": cmat,
            "t0": np.array([[T0]], np.int32),
        })
    return maps


_CACHE = {}


AB_KEYS = ("cvec", "cmat", "xT", "xTown", "wh", "wconv", "wpw2", "gpre", "convw")
C_KEYS = ("cvec", "cmat", "xown", "wout", "gpost")


def kernel(x, g_pre, w_in, conv_w, conv_b, ln_g, ln_b, w_pw2, b_pw2, w_out, g_post):
    S = int(np.asarray(x).shape[1])
    TOK = S // NCORES
    maps = make_in_maps(x, g_pre, w_in, conv_w, conv_b, ln_g, ln_b, w_pw2, b_pw2, w_out, g_post)
    cores = list(range(NCORES))
    nc1 = build_program(S, "ab")
    r1 = run_bass_kernel_spmd(nc1, [{k: m[k] for k in AB_KEYS} for m in maps], core_ids=cores)
    ya = [np.asarray(r["yatt_loc"]) for r in r1.results]
    maps2 = []
    for c in range(NCORES):
        m = {k: maps[c][k] for k in C_KEYS}
        yin = np.stack([ya[h][:, c * TOK:(c + 1) * TOK] for h in range(NCORES)], axis=1)
        m["yatt_in"] = np.ascontiguousarray(yin.reshape(128, 8 * TOK))
        m["yconv_d"] = np.ascontiguousarray(np.asarray(r1.results[c]["yconv_d"]))
        maps2.append(m)
    nc2 = build_program(S, "c")
    r2 = run_bass_kernel_spmd(nc2, maps2, core_ids=cores)
    outs = [np.asarray(r["out"], dtype=np.float32) for r in r2.results]
    return np.concatenate(outs, axis=0)[None, :, :]
```

```python
import numpy as np
import concourse.bass as bass
import concourse.mybir as mybir
from concourse.bass_utils import run_bass_kernel_spmd

F32 = mybir.dt.float32
BF16 = mybir.dt.bfloat16
I32 = mybir.dt.int32
AF = mybir.ActivationFunctionType
ALU = mybir.AluOpType
AX = mybir.AxisListType

NCORES = 8
DM = 2048
KC = DM // 128
CW = 1024
CK = 31
RMS_EPS = 1e-6
LN_EPS = 1e-5
QSCALE = 1.0 / float(np.sqrt(128.0))
NEGBIG = -30000.0
C_ONES, C_NEGONES, C_NEGTRI, C_IDENT, C_NEGBIG, C_INVMASK, C_ZEROS = range(7)
NCONST = 7


class _Op:
    __slots__ = ("eng", "fn", "dma", "inc", "deps", "signal", "cnt", "val", "idx")


class Sched:
    COMPUTE = ("pe", "act", "dve", "pool")

    def __init__(self, nc, sems):
        self.nc = nc
        self.sems = sems
        self.ops = []
        self.lw = {}
        self.rd = {}
        self.cnt = {e: 0 for e in self.COMPUTE}
        self.dman = {}
        self.waited = {e: {} for e in ("pe", "act", "dve", "pool", "sp")}
        self.nops = 0

    def add(self, eng, fn, r=(), w=(), dma=None, inc=16):
        o = _Op()
        o.eng = eng; o.fn = fn; o.dma = dma; o.inc = inc
        o.signal = False; o.cnt = None; o.val = None
        o.idx = self.nops; self.nops += 1
        deps = {}
        for k in r:
            d = self.lw.get(k)
            if d is not None:
                deps[d.idx] = d
        for k in w:
            d = self.lw.get(k)
            if d is not None:
                deps[d.idx] = d
            for d2 in self.rd.get(k, {}).values():
                deps[d2.idx] = d2
        deps.pop(o.idx, None)
        o.deps = [d for d in deps.values()
                  if not (d.eng == "pe" and eng == "pe" and d.dma is None and dma is None)]
        for d in o.deps:
            d.signal = True
        for k in r:
            rk = self.rd.setdefault(k, {})
            rk[eng if dma is None else ("dma", o.idx)] = o
        for k in w:
            self.lw[k] = o
            self.rd[k] = {}
        self.ops.append(o)
        return o

    def flush(self):
        nc = self.nc
        ops = self.ops
        last = {}
        for o in ops:
            if o.dma is None:
                last[o.eng] = o
        for o in last.values():
            o.signal = True
        for o in ops:
            if o.dma is not None:
                v = self.dman.get(o.dma, 0) + o.inc
                self.dman[o.dma] = v
                o.val = v
            elif o.signal:
                self.cnt[o.eng] += 1
                o.cnt = self.cnt[o.eng]
        fin_cnt = dict(self.cnt)
        fin_dma = dict(self.dman)
        sems = self.sems
        waited = self.waited

        def emit(eng_name, e):
            wd = waited[eng_name]
            for o in ops:
                if o.eng != eng_name:
                    continue
                for d in o.deps:
                    if d.dma is not None:
                        s, v = d.dma, d.val
                    else:
                        s, v = d.eng, d.cnt
                    if wd.get(s, 0) >= v:
                        continue
                    e.wait_ge(sems[s], v)
                    wd[s] = v
                ins = o.fn(e)
                if o.dma is not None:
                    ins.then_inc(sems[o.dma], o.inc)
                elif o.signal:
                    ins.then_inc(sems[o.eng], 1)
            for s, v in list(fin_cnt.items()) + list(fin_dma.items()):
                if v > 0 and wd.get(s, 0) < v:
                    e.wait_ge(sems[s], v)
                    wd[s] = v

        with nc.Block() as block:
            @block.tensor
            def _(e):
                emit("pe", e)

            @block.scalar
            def _(e):
                emit("act", e)

            @block.vector
            def _(e):
                emit("dve", e)

            @block.gpsimd
            def _(e):
                emit("pool", e)

            @block.sync
            def _(e):
                emit("sp", e)
        self.ops = []
        self.lw = {}
        self.rd = {}


def _split(lo, hi, step):
    out = []
    while lo < hi:
        out.append((lo, min(hi, lo + step)))
        lo += step
    return out


def _mm(out, lhsT, rhs, start, stop, skip=False):
    if skip:
        return lambda e: e.matmul(out, lhsT=lhsT, rhs=rhs, start=start, stop=stop, skip_group_check=True)
    return lambda e: e.matmul(out, lhsT=lhsT, rhs=rhs, start=start, stop=stop)


def _act(out, in_, func, scale=1.0, bias=0.0):
    return lambda e: e.activation(out=out, in_=in_, func=func, bias=bias, scale=scale)


def _tt(out, a, b, op):
    return lambda e: e.tensor_tensor(out=out, in0=a, in1=b, op=op)


def _ts(out, a, s1, op0, s2=None, op1=None):
    if op1 is None:
        return lambda e: e.tensor_scalar(out=out, in0=a, scalar1=s1, scalar2=None, op0=op0)
    return lambda e: e.tensor_scalar(out=out, in0=a, scalar1=s1, scalar2=s2, op0=op0, op1=op1)


def _stt(out, a, s, b, op0, op1):
    return lambda e: e.scalar_tensor_tensor(out=out, in0=a, scalar=s, in1=b, op0=op0, op1=op1)


def _cp(out, in_):
    return lambda e: e.tensor_copy(out=out, in_=in_)


def _rcp(out, in_):
    return lambda e: e.reciprocal(out=out, in_=in_)


def _dma(out, in_):
    return lambda e: e.dma_start(out=out, in_=in_)


def _memset(ap, v):
    return lambda e: e.memset(ap, v)


def build_program(S, mode="fused"):
    TOK = S // NCORES
    NT = S // 512
    HALF = min(1024, TOK)
    NH = TOK // HALF
    HW = HALF + 32
    NBLK = S // 128

    nc = bass.Bass("TRN2", target_bir_lowering=False)
    din = lambda n, sh, dt=F32: nc.dram_tensor(n, sh, dt, kind="ExternalInput").ap()
    need_ab = mode != "c"
    need_c = mode != "ab"
    cvec = din("cvec", [128, 32])
    cmat = din("cmat", [128, NCONST * 128])
    if need_ab:
        xT = din("xT", [DM, S])
        xTown = din("xTown", [DM, 32 + TOK])
        wh = din("wh", [DM, 512])
        wconv = din("wconv", [8 * 3 * 128, DM])
        wpw2 = din("wpw2", [CW, CW])
        gpre = din("gpre", [128, KC])
        convw = din("convw", [128, 8 * CK])
    if need_c:
        xown = din("xown", [TOK, DM])
        wout = din("wout", [DM, DM])
        gpost = din("gpost", [128, DM])
    if mode == "fused":
        t0in = din("t0", [1, 1], I32)
    if mode == "ab":
        yatt_loc = nc.dram_tensor("yatt_loc", [128, S], BF16, kind="ExternalOutput")
        yconv_d = nc.dram_tensor("yconv_d", [128, 8 * TOK], BF16, kind="ExternalOutput")
        yatt_all = None
        out = None
    elif mode == "c":
        out = nc.dram_tensor("out", [TOK, DM], F32, kind="ExternalOutput").ap()
        yatt_loc = None
        yatt_all = nc.dram_tensor("yatt_in", [128, 8 * TOK], BF16, kind="ExternalInput")
        yconv_d = nc.dram_tensor("yconv_d", [128, 8 * TOK], BF16, kind="ExternalInput")
    else:
        out = nc.dram_tensor("out", [TOK, DM], F32, kind="ExternalOutput").ap()
        yatt_loc = nc.dram_tensor("yatt_loc", [128, S], BF16)
        yatt_all = nc.dram_tensor("yatt_all", [128 * NCORES, S], BF16)
        yconv_d = nc.dram_tensor("yconv_d", [128, 8 * TOK], BF16)

    if need_ab:
        xT_v = xT.rearrange("(kc p) t -> p kc t", p=128)
        xTown_v = xTown.rearrange("(kc p) t -> p kc t", p=128)
    yconv_v = yconv_d.ap().rearrange("p (c t) -> p c t", c=8)
    if mode == "c":
        yall_v = yatt_all.ap().rearrange("p (r t) -> p r t", r=8)
    elif mode == "fused":
        yall_v = yatt_all.ap().rearrange("(r p) t -> p r t", p=128)

    import contextlib
    es = contextlib.ExitStack()
    with es:
        def sb(name, shape, dt=F32):
            return es.enter_context(nc.sbuf_tensor(name, shape, dt))

        sem_names = ["pe", "act", "dve", "pool", "init", "xs0", "xs1", "xs2", "ws0", "ws1",
                     "yo0", "yo1", "ycd", "cc", "ya0", "ya1", "yc0", "yc1", "xr0", "xr1",
                     "o0", "o1"]
        sems = {n: es.enter_context(nc.semaphore("s_" + n)) for n in sem_names}
        PS = [None] * 8
        greg = es.enter_context(nc.gpsimd.register("greg"))
        sch = Sched(nc, sems)
        A = sch.add

        cstage = sb("cstage", [128, NCONST * 128])
        cb = sb("cb", [128, NCONST, 128], BF16)
        gpre_sb = sb("gpre_sb", [128, KC])
        convw_sb = sb("convw_sb", [128, 8, CK])
        cvec_sb = sb("cvec_sb", [128, 4, 8])
        ncvec_sb = sb("ncvec_sb", [128, 4, 8])
        xs = [sb("xs%d" % i, [128, 2, 512]) for i in range(3)]
        sq = [sb("sq%d" % i, [128, 2, 512], BF16) for i in range(2)]
        TT = {}

        def tmp(name, dt=F32, n=512):
            if name not in TT:
                TT[name] = sb("t_" + name, [128, n], dt)
            return TT[name]

        ones_bf = cb[:, C_ONES, :]
        negones_bf = cb[:, C_NEGONES, :]
        negtri_bf = cb[:, C_NEGTRI, :]
        ident_bf = cb[:, C_IDENT, :]
        negbig_bf = cb[:, C_NEGBIG, :]
        invmask_bf = cb[:, C_INVMASK, :]
        zeros_bf = cb[:, C_ZEROS, :]
        ones_f = cstage[:, C_ONES * 128:(C_ONES + 1) * 128]

        A("sp", _dma(cstage[:], cmat[:, :]), w=["cstage"], dma="init")
        if need_ab:
            A("sp", _dma(gpre_sb[:], gpre[:, :]), w=["gpre"], dma="init")
            A("sp", _dma(convw_sb[:], convw.rearrange("p (c k) -> p c k", c=8)), w=["convw"], dma="init")
        A("sp", _dma(cvec_sb[:], cvec.rearrange("p (a c) -> p a c", a=4)), w=["cvec"], dma="init")
        sch.flush()
        A("dve", _cp(cb[:].rearrange("p a b -> p (a b)"), cstage[:]), w=["cb"])
        A("dve", _ts(ncvec_sb[:].rearrange("p a b -> p (a b)"),
                     cvec_sb[:].rearrange("p a b -> p (a b)"), -1.0, ALU.mult), w=["ncvec"])

        xs_ctr = [0]
        sq_ctr = [0]

        def load_x_piece(src_ap, w, xb_dst_fn, ps_ap, first, last, pc):
            sl = xs_ctr[0] % 3
            xs_ctr[0] += 1
            ssl = sq_ctr[0] % 2
            sq_ctr[0] += 1
            A("sp", _dma(xs[sl][:, :, 0:w], src_ap), w=[("xs", sl)], dma="xs%d" % sl)
            A("dve", _tt(sq[ssl][:, :, 0:w], xs[sl][:, :, 0:w], xs[sl][:, :, 0:w], ALU.mult),
              r=[("xs", sl)], w=[("sq", ssl)])
            for kk in range(2):
                kc = 2 * pc + kk
                A("dve", _ts(xb_dst_fn(kc), xs[sl][:, kk, 0:w], gpre_sb[:, kc:kc + 1], ALU.mult),
                  r=[("xs", sl), "gpre"], w=[("xb", kc)])
            for kk in range(2):
                A("pe", _mm(ps_ap, ones_bf, sq[ssl][:, kk, 0:w],
                            first and kk == 0, last and kk == 1),
                  r=[("sq", ssl), "cb"], w=[("ps", 7)])

        esA = contextlib.ExitStack()
        for _ph in ([0] if mode != "c" else []):
            def sbA(name, shape, dt=F32):
                return esA.enter_context(nc.sbuf_tensor(name, shape, dt))

            for _i in range(8):
                PS[_i] = esA.enter_context(nc.psum_tensor("psA%d" % _i, [128, 512], F32))
            xbA = sbA("xbA", [128, KC, HW], BF16)
            rstdA = sbA("rstdA", [128, HW])
            cA = sbA("cA", [128, 8, HALF])
            hA = [sbA("hA%d" % i, [128, HW], BF16) for i in range(2)]
            Dk = sbA("Dk", [128, CK, 128], BF16)
            wst = [sbA("wst%d" % i, [128, KC * 128]) for i in range(2)]
            wcb = [sbA("wcb%d" % i, [128, 3, KC * 128], BF16) for i in range(2)]
            pw2b = sbA("pw2b", [128, 8, CW], BF16)
            ycs = sbA("ycs", [128, 8, HALF], BF16)
            hs = sbA("hs", [128, 8, 512], BF16)
            tA = {n: sbA("tA_" + n, [128, 512]) for n in
                  ("ln", "t1", "t2", "t3", "t4", "t5", "tm", "tq", "trs", "csq0", "csq1")}
            ws_ctr = [0]
            tctr = [0]
            cctr = [0]

            def load_w(src_ap, dst_ap, eng):
                sl = ws_ctr[0] % 2
                ws_ctr[0] += 1
                A("sp", _dma(wst[sl][:], src_ap), w=[("wst", sl)], dma="ws%d" % sl)
                A(eng, _cp(dst_ap, wst[sl][:]), r=[("wst", sl)], w=["wdst"])

            wpw2_v = wpw2.rearrange("(ci p) co -> p ci co", p=128)
            for ci2 in range(4):
                sl = ws_ctr[0] % 2
                ws_ctr[0] += 1
                A("sp", _dma(wst[sl][:].rearrange("p (a b) -> p a b", a=2),
                             wpw2_v[:, 2 * ci2:2 * ci2 + 2, :]), w=[("wst", sl)], dma="ws%d" % sl)
                A("dve", _cp(pw2b[:, 2 * ci2:2 * ci2 + 2, :],
                             wst[sl][:].rearrange("p (a b) -> p a b", a=2)),
                  r=[("wst", sl)], w=["pw2b"])

            for hf in range(NH):
                base = hf * HALF
                tiles = _split(0, HW, 512)
                for (lo, hi) in tiles:
                    w = hi - lo
                    for pc in range(8):
                        load_x_piece(xTown_v[:, 2 * pc:2 * pc + 2, base + lo:base + hi], w,
                                     lambda kc, lo=lo, hi=hi: xbA[:, kc, lo:hi],
                                     PS[7][:, 0:w], pc == 0, pc == 7, pc)
                    A("act", _act(tA["ln"][:, 0:w], PS[7][:, 0:w], AF.Ln, scale=1.0 / DM, bias=RMS_EPS),
                      r=[("ps", 7)], w=["tA_ln"])
                    A("act", _act(rstdA[:, lo:hi], tA["ln"][:, 0:w], AF.Exp, scale=-0.5),
                      r=["tA_ln"], w=["rstdA"])
                for i in range(8):
                    wsl = i % 2
                    for kind in range(3):
                        row0 = (i * 3 + kind) * 128
                        sl = ws_ctr[0] % 2
                        ws_ctr[0] += 1
                        A("sp", _dma(wst[sl][:], wconv[row0:row0 + 128, :]), w=[("wst", sl)],
                          dma="ws%d" % sl)
                        A("dve" if kind != 1 else "pool", _cp(wcb[wsl][:, kind, :], wst[sl][:]),
                          r=[("wst", sl)], w=[("wcb", wsl, kind)])
                    hsl = i % 2
                    for (lo, hi) in tiles:
                        w = hi - lo
                        bset = 3 * (tctr[0] % 2)
                        tctr[0] += 1
                        for kind in range(3):
                            for kc in range(KC):
                                A("pe", _mm(PS[bset + kind][:, 0:w], wcb[wsl][:, kind, kc * 128:(kc + 1) * 128],
                                            xbA[:, kc, lo:hi], kc == 0, kc == KC - 1),
                                  r=[("wcb", wsl, kind), ("xb", kc)], w=[("ps", bset + kind)])
                        A("dve", _tt(tA["t1"][:, 0:w], PS[bset + 1][:, 0:w], rstdA[:, lo:hi], ALU.mult),
                          r=[("ps", bset + 1), "rstdA"], w=["t1"])
                        A("act", _act(tA["t2"][:, 0:w], tA["t1"][:, 0:w], AF.Exp, scale=-1.0),
                          r=["t1"], w=["t2"])
                        A("act", _act(tA["t2"][:, 0:w], tA["t2"][:, 0:w], AF.Ln, bias=1.0), r=["t2"], w=["t2"])
                        A("act", _act(tA["t2"][:, 0:w], tA["t2"][:, 0:w], AF.Exp, scale=-1.0), r=["t2"], w=["t2"])
                        A("dve", _tt(tA["t3"][:, 0:w], PS[bset][:, 0:w], rstdA[:, lo:hi], ALU.mult),
                          r=[("ps", bset), "rstdA"], w=["t3"])
                        A("pool", _tt(hA[hsl][:, lo:hi], tA["t3"][:, 0:w], tA["t2"][:, 0:w], ALU.mult),
                          r=["t3", "t2"], w=[("hA", hsl)])
                        lo2 = max(lo, 32)
                        if lo2 < hi:
                            w2 = hi - lo2
                            o2 = lo2 - lo
                            A("dve", _tt(tA["t4"][:, 0:w2], PS[bset + 2][:, o2:w], rstdA[:, lo2:hi], ALU.mult),
                              r=[("ps", bset + 2), "rstdA"], w=["t4"])
                            A("act", _act(tA["t5"][:, 0:w2], tA["t4"][:, 0:w2], AF.Exp, scale=-1.0),
                              r=["t4"], w=["t5"])
                            A("act", _act(tA["t5"][:, 0:w2], tA["t5"][:, 0:w2], AF.Ln, bias=1.0),
                              r=["t5"], w=["t5"])
                            A("act", _act(tA["t5"][:, 0:w2], tA["t5"][:, 0:w2], AF.Exp, scale=-1.0),
                              r=["t5"], w=["t5"])
                            A("pool", _tt(ycs[:, i, lo2 - 32:hi - 32], tA["t4"][:, 0:w2],
                                          tA["t5"][:, 0:w2], ALU.mult),
                              r=["t4", "t5"], w=[("ycs", i)])
                    for k in range(CK):
                        A("dve", _ts(Dk[:, k, :], cstage[:, C_IDENT * 128:(C_IDENT + 1) * 128],
                                     convw_sb[:, i, k:k + 1], ALU.mult),
                          r=["cstage", "convw"], w=["Dk"])
                    for (lo, hi) in _split(0, HALF, 512):
                        w = hi - lo
                        cb_ = 6 + cctr[0] % 2
                        cctr[0] += 1
                        for k in range(CK):
                            A("pe", _mm(PS[cb_][:, 0:w], Dk[:, k, :], hA[hsl][:, lo + k + 2:hi + k + 2],
                                        k == 0, k == CK - 1),
                              r=["Dk", ("hA", hsl)], w=[("ps", cb_)])
                        A("act", _act(cA[:, i, lo:hi], PS[cb_][:, 0:w], AF.Identity,
                                      bias=cvec_sb[:, 0, i:i + 1]),
                          r=[("ps", cb_), "cvec"], w=[("cA", i)])
                for (lo, hi) in _split(0, HALF, 512):
                    w = hi - lo
                    for i in range(8):
                        A("pe", _mm(PS[4][:, 0:w], ones_f, cA[:, i, lo:hi], i == 0, i == 7),
                          r=[("cA", i), "cstage"], w=[("ps", 4)])
                    for i in range(8):
                        cs = "csq%d" % (i % 2)
                        A("act", _act(tA[cs][:, 0:w], cA[:, i, lo:hi], AF.Square), r=[("cA", i)], w=[cs])
                        A("pe", _mm(PS[5][:, 0:w], ones_f, tA[cs][:, 0:w], i == 0, i == 7),
                          r=[cs, "cstage"], w=[("ps", 5)])
                    A("dve", _ts(tA["tm"][:, 0:w], PS[4][:, 0:w], 1.0 / CW, ALU.mult), r=[("ps", 4)], w=["tm"])
                    A("dve", _tt(tA["tq"][:, 0:w], tA["tm"][:, 0:w], tA["tm"][:, 0:w], ALU.mult),
                      r=["tm"], w=["tq"])
                    A("dve", _stt(tA["tq"][:, 0:w], PS[5][:, 0:w], 1.0 / CW, tA["tq"][:, 0:w],
                                  ALU.mult, ALU.subtract), r=[("ps", 5), "tq"], w=["tq"])
                    A("act", _act(tA["tq"][:, 0:w], tA["tq"][:, 0:w], AF.Ln, bias=LN_EPS), r=["tq"], w=["tq"])
                    A("act", _act(tA["trs"][:, 0:w], tA["tq"][:, 0:w], AF.Exp, scale=-0.5), r=["tq"], w=["trs"])
                    for i in range(8):
                        A("dve", _tt(tA["t1"][:, 0:w], cA[:, i, lo:hi], tA["tm"][:, 0:w], ALU.subtract),
                          r=[("cA", i), "tm"], w=["t1"])
                        A("pool", _tt(tA["t1"][:, 0:w], tA["t1"][:, 0:w], tA["trs"][:, 0:w], ALU.mult),
                          r=["t1", "trs"], w=["t1"])
                        A("act", _act(tA["t2"][:, 0:w], tA["t1"][:, 0:w], AF.Exp,
                                      scale=ncvec_sb[:, 1, i:i + 1], bias=ncvec_sb[:, 2, i:i + 1]),
                          r=["t1", "ncvec"], w=["t2"])
                        A("dve", _ts(tA["t3"][:, 0:w], tA["t1"][:, 0:w], cvec_sb[:, 1, i:i + 1], ALU.mult,
                                     cvec_sb[:, 2, i:i + 1], ALU.add), r=["t1", "cvec"], w=["t3"])
                        A("act", _act(tA["t2"][:, 0:w], tA["t2"][:, 0:w], AF.Ln, bias=1.0), r=["t2"], w=["t2"])
                        A("act", _act(tA["t2"][:, 0:w], tA["t2"][:, 0:w], AF.Exp, scale=-1.0), r=["t2"], w=["t2"])
                        A("pool", _tt(hs[:, i, 0:w], tA["t3"][:, 0:w], tA["t2"][:, 0:w], ALU.mult),
                          r=["t3", "t2"], w=[("hs", i)])
                    for co in range(8):
                        bank = 2 + co % 2
                        for ci in range(8):
                            A("pe", _mm(PS[bank][:, 0:w], pw2b[:, ci, co * 128:(co + 1) * 128],
                                        hs[:, ci, 0:w], ci == 0, ci == 7),
                              r=["pw2b", ("hs", ci)], w=[("ps", bank)])
                        A("dve", _stt(ycs[:, co, lo:hi], PS[bank][:, 0:w], cvec_sb[:, 3, co:co + 1],
                                      ycs[:, co, lo:hi], ALU.add, ALU.mult),
                          r=[("ps", bank), ("ycs", co), "cvec"], w=[("ycs", co)])
                A("sp", _dma(yconv_v[:, :, base:base + HALF], ycs[:]),
                  r=[("ycs", i) for i in range(8)], w=["yconv_d"], dma="ycd")
            sch.flush()
            esA.close()

        esB = contextlib.ExitStack()
        for _ph in ([0] if mode != "c" else []):
            def sbB(name, shape, dt=F32):
                return esB.enter_context(nc.sbuf_tensor(name, shape, dt))

            ZP = [esB.enter_context(nc.psum_tensor("zp%d" % _i, [128, 1024], F32)) for _i in range(3)]
            OB = esB.enter_context(nc.psum_tensor("ob", [128, 512], F32))
            PS[7] = esB.enter_context(nc.psum_tensor("pj", [128, 512], F32))
            kT = sbB("kT", [128, S], BF16)
            vv = sbB("vv", [128, NBLK, 128], BF16)
            whb = sbB("whb", [128, KC, 512], BF16)
            whst = sbB("whst", [128, 4, 512])
            xb = sbB("xb", [128, KC, 512], BF16)
            qT = [sbB("qT%d" % i, [128, 512], BF16) for i in range(2)]
            szt = [sbB("sz%d" % i, [128, 512]) for i in range(2)]
            vTt = sbB("vTt", [128, 512], BF16)
            rstd = sbB("rstd", [128, 512])
            tB = {n: sbB("tB_" + n, [128, 512]) for n in ("ln", "tz", "te")}
            Et = [sbB("E%d" % i, [128, 2, 512]) for i in range(3)]
            spb = [sbB("spb%d" % i, [128, 2, 512], BF16) for i in range(3)]
            Rt = sbB("R", [128, 512])
            Rb = [sbB("Rb%d" % i, [128, 512], BF16) for i in range(2)]
            ab = [sbB("ab%d" % i, [128, 2, 512], BF16) for i in range(2)]
            yo = [sbB("yo%d" % i, [128, 512], BF16) for i in range(2)]

            wh_v = wh.rearrange("(kc p) n -> p kc n", p=128)
            for q4 in range(4):
                A("sp", _dma(whst[:], wh_v[:, 4 * q4:4 * q4 + 4, :]), w=["whst"], dma="ws0")
                A("dve", _cp(whb[:, 4 * q4:4 * q4 + 4, :], whst[:]), r=["whst"], w=["whb"])

            def proj_gen(j):
                t0 = 512 * j
                qs = j % 2
                for pc in range(8):
                    load_x_piece(xT_v[:, 2 * pc:2 * pc + 2, t0:t0 + 512], 512,
                                 lambda kc: xb[:, kc, :], PS[7][:, :], pc == 0, pc == 7, pc)
                    yield
                A("act", _act(tB["ln"][:], PS[7][:], AF.Ln, scale=1.0 / DM, bias=RMS_EPS),
                  r=[("ps", 7)], w=["tB_ln"])
                A("act", _act(rstd[:], tB["ln"][:], AF.Exp, scale=-0.5), r=["tB_ln"], w=["rstd"])
                yield

                def chain(which):
                    for kc in range(KC):
                        A("pe", _mm(PS[7][:, :], whb[:, kc, which * 128:(which + 1) * 128], xb[:, kc, :],
                                    kc == 0, kc == KC - 1),
                          r=["whb", ("xb", kc)], w=[("ps", 7)])
                        if kc % 4 == 3 and kc != KC - 1:
                            yield
                yield from chain(0)
                A("dve", _stt(qT[qs][:], PS[7][:], QSCALE, rstd[:], ALU.mult, ALU.mult),
                  r=[("ps", 7), "rstd"], w=[("qT", qs)])
                yield
                yield from chain(1)
                A("dve", _tt(kT[:, t0:t0 + 512], PS[7][:], rstd[:], ALU.mult),
                  r=[("ps", 7), "rstd"], w=[("kT", 4 * j + b) for b in range(4)])
                yield
                yield from chain(2)
                A("dve", _tt(vTt[:], PS[7][:], rstd[:], ALU.mult), r=[("ps", 7), "rstd"], w=["vTt"])
                for b in range(4):
                    A("pe", _mm(PS[7][:, b * 128:(b + 1) * 128], vTt[:, b * 128:(b + 1) * 128], ident_bf,
                                True, True), r=["vTt", "cb"], w=[("ps", 7)])
                A("dve", _cp(vv[:, 4 * j:4 * j + 4, :].rearrange("p a b -> p (a b)"), PS[7][:]),
                  r=[("ps", 7)], w=[("v", 4 * j + b) for b in range(4)])
                yield
                yield from chain(3)
                A("dve", _tt(tB["tz"][:], PS[7][:], rstd[:], ALU.mult), r=[("ps", 7), "rstd"], w=["tz"])
                A("act", _act(tB["te"][:], tB["tz"][:], AF.Exp, scale=-1.0), r=["tz"], w=["te"])
                A("dve", _ts(tB["te"][:], tB["te"][:], 1.0, ALU.add), r=["te"], w=["te"])
                A("dve", _rcp(tB["te"][:], tB["te"][:]), r=["te"], w=["te"])
                A("pool", _tt(szt[qs][:], tB["tz"][:], tB["te"][:], ALU.mult), r=["tz", "te"], w=[("sz", qs)])
                yield

            units = []
            for j in range(NT):
                lst = [[(4 * j + r, 128 * r)] for r in (3, 2, 1, 0)]
                kbs = list(range(4 * j - 1, -1, -1))
                for p in range(0, len(kbs), 2):
                    lst.append([(kbs[p], 0), (kbs[p + 1], 0)])
                for idx, subs in enumerate(lst):
                    last = idx == len(lst) - 1
                    units.append(dict(j=j, subs=subs, diag=idx < 4, first=idx == 0, last=last,
                                      c0n=None if last else lst[idx + 1][0][1]))
            G = len(units)

            def U1pe(u):
                un = units[u]
                qs = un["j"] % 2
                zb = u % 3
                for si, (kb, c0) in enumerate(un["subs"]):
                    A("pe", _mm(ZP[zb][:, si * 512 + c0:(si + 1) * 512], kT[:, kb * 128:(kb + 1) * 128],
                                qT[qs][:, c0:512], True, not un["diag"]),
                      r=[("kT", kb), ("qT", qs)], w=[("zp", zb, si)])
                    if un["diag"]:
                        A("pe", _mm(ZP[zb][:, c0:c0 + 128], negbig_bf, invmask_bf, False, True),
                          r=["cb"], w=[("zp", zb, si)])

            def U1act(u):
                un = units[u]
                zb = u % 3
                eb = u % 3
                if len(un["subs"]) == 2:
                    A("act", _act(Et[eb][:].rearrange("p a b -> p (a b)"), ZP[zb][:, :], AF.Exp),
                      r=[("zp", zb, 0), ("zp", zb, 1)], w=[("E", eb)])
                else:
                    c0 = un["subs"][0][1]
                    A("act", _act(Et[eb][:, 0, c0:512], ZP[zb][:, c0:512], AF.Exp),
                      r=[("zp", zb, 0)], w=[("E", eb)])

            def U2act(u):
                un = units[u]
                eb = u % 3
                if len(un["subs"]) == 2:
                    A("act", _act(spb[eb][:].rearrange("p a b -> p (a b)"),
                                  Et[eb][:].rearrange("p a b -> p (a b)"), AF.Ln, bias=1.0),
                      r=[("E", eb)], w=[("sp", eb)])
                else:
                    c0 = un["subs"][0][1]
                    A("act", _act(spb[eb][:, 0, c0:512], Et[eb][:, 0, c0:512], AF.Ln, bias=1.0),
                      r=[("E", eb)], w=[("sp", eb)])

            def U2pe(u):
                un = units[u]
                eb = u % 3
                zb = u % 3
                for si, (kb, c0) in enumerate(un["subs"]):
                    Lr = ZP[zb][:, si * 512 + c0:(si + 1) * 512]
                    ops = [(negtri_bf, spb[eb][:, si, c0:512], ["cb", ("sp", eb)])]
                    if not un["first"]:
                        ops.append((negones_bf, Rb[u % 2][:, c0:512], ["cb", ("Rb", u % 2)]))
                    if si == 1:
                        ops.append((negones_bf, spb[eb][:, 0, :], ["cb", ("sp", eb)]))
                    for oi, (lh, rh, rk) in enumerate(ops):
                        A("pe", _mm(Lr, lh, rh, False, oi == len(ops) - 1, skip=True),
                          r=rk + [("E", eb)], w=[("zp", zb, si)])
                if un["first"]:
                    A("pool", _memset(Rt[:], 0.0), w=["R"])
                if not un["last"]:
                    for si, (kb, c0) in enumerate(un["subs"]):
                        A("pool", _tt(Rt[:, c0:512], Rt[:, c0:512], spb[eb][:, si, c0:512], ALU.add),
                          r=["R", ("sp", eb)], w=["R"])
                    c0n = un["c0n"]
                    A("dve", _cp(Rb[(u + 1) % 2][:, c0n:512], Rt[:, c0n:512]),
                      r=["R"], w=[("Rb", (u + 1) % 2)])

            def U3act(u):
                un = units[u]
                zb = u % 3
                if len(un["subs"]) == 2:
                    A("act", _act(ab[u % 2][:].rearrange("p a b -> p (a b)"), ZP[zb][:, :], AF.Exp),
                      r=[("zp", zb, 0), ("zp", zb, 1)], w=[("a", u % 2)])
                else:
                    c0 = un["subs"][0][1]
                    A("act", _act(ab[u % 2][:, 0, c0:512], ZP[zb][:, c0:512], AF.Exp),
                      r=[("zp", zb, 0)], w=[("a", u % 2)])

            def U3pe(u):
                un = units[u]
                j = un["j"]
                qs = j % 2
                nsub = len(un["subs"])
                if un["first"]:
                    A("pe", _mm(OB[:, :], zeros_bf, qT[qs][:, :], True, False),
                      r=["cb", ("qT", qs)], w=["ob"])
                for si, (kb, c0) in enumerate(un["subs"]):
                    A("pe", _mm(OB[:, c0:512], vv[:, kb, :], ab[u % 2][:, si, c0:512], False,
                                un["last"] and si == nsub - 1),
                      r=[("v", kb), ("a", u % 2)], w=["ob"])
                if un["last"]:
                    A("dve", _tt(yo[qs][:], OB[:, :], szt[qs][:], ALU.mult),
                      r=["ob", ("sz", qs)], w=[("yo", qs)])
                    A("sp", _dma(yatt_loc[:, 512 * j:512 * j + 512], yo[qs][:]),
                      r=[("yo", qs)], w=["yatt_loc"], dma="yo%d" % qs)

            gen = proj_gen(0)
            for _ in gen:
                pass
            gen = proj_gen(1) if NT > 1 else iter(())
            gen_j = 1

            def ensure_proj(i):
                nonlocal gen, gen_j
                if i < G and units[i]["first"] and units[i]["j"] == gen_j:
                    for _ in gen:
                        pass
                    gen_j += 1
                    gen = proj_gen(gen_j) if gen_j < NT else iter(())

            U1pe(0)
            for it in range(G + 2):
                if 0 <= it - 1 < G:
                    U2act(it - 1)
                if it < G:
                    U1act(it)
                if 0 <= it - 2 < G:
                    U3act(it - 2)
                if it + 1 < G:
                    ensure_proj(it + 1)
                    U1pe(it + 1)
                if 0 <= it - 1 < G:
                    U2pe(it - 1)
                if 0 <= it - 2 < G:
                    U3pe(it - 2)
                next(gen, None)
            sch.flush()
            esB.close()

        esC = contextlib.ExitStack()
        for _ph in ([0] if mode != "ab" else []):
            def sbC(name, shape, dt=F32):
                return esC.enter_context(nc.sbuf_tensor(name, shape, dt))

            for _i in range(8):
                PS[_i] = esC.enter_context(nc.psum_tensor("psC%d" % _i, [128, 512], F32))
            woutb = sbC("woutb", [128, KC, DM], BF16)
            wst2 = [sbC("wst2_%d" % i, [128, DM]) for i in range(2)]
            ycat = [sbC("ycat%d" % i, [128, KC, 512], BF16) for i in range(2)]
            xres = [sbC("xres%d" % i, [128, DM]) for i in range(2)]
            ot = [sbC("ot%d" % i, [128, DM]) for i in range(2)]
            gpost_sb = sbC("gpost_sb", [128, DM])
            junk = sbC("junk", [128, DM])
            ss1 = sbC("ss1", [128, 1])
            rs1 = sbC("rs1", [128, 1])

            def coll(e):
                return e.collective_compute("AllGather", ALU.bypass,
                                            replica_groups=[list(range(NCORES))],
                                            ins=[yatt_loc.ap().opt()], outs=[yatt_all.ap().opt()])
            if mode == "fused":
                A("pool", coll, r=["yatt_loc"], w=["yatt_all"], dma="cc", inc=1)
            A("sp", _dma(gpost_sb[:], gpost[:, :]), w=["gpost"], dma="init")
            for kc in range(KC):
                sl = kc % 2
                A("sp", _dma(wst2[sl][:], wout[kc * 128:(kc + 1) * 128, :]), w=[("wst2", sl)],
                  dma="ws%d" % sl)
                A("dve", _cp(woutb[:, kc, :], wst2[sl][:]),
                  r=[("wst2", sl)], w=["woutb"])

            state = {"val": None}

            def load_ya(dst, tb4):
                def fn(e):
                    if mode == "c":
                        return e.dma_start(out=dst, in_=yall_v[:, :, tb4 * 512:tb4 * 512 + 512])
                    if state["val"] is None:
                        e.reg_load(greg, t0in[0:1, 0:1])
                        state["val"] = e.snap(greg)
                    src = yall_v[:, :, tb4 * 512:]
                    return e.dma_start(out=dst, in_=src[:, :, bass.ds(state["val"], 512)])
                return fn

            NTB = TOK // 128
            for tb in range(NTB):
                tb4, within = tb // 4, tb % 4
                ysl = tb4 % 2
                if within == 0:
                    A("pool", load_ya(ycat[ysl][:, 0:8, :], tb4), r=["yatt_all"], w=[("ycat", ysl, 0)],
                      dma="ya%d" % ysl)
                    A("sp", _dma(ycat[ysl][:, 8:16, :], yconv_v[:, :, tb4 * 512:tb4 * 512 + 512]),
                      w=[("ycat", ysl, 1)], dma="yc%d" % ysl)
                xsl = tb % 2
                A("sp", _dma(xres[xsl][:], xown[tb * 128:(tb + 1) * 128, :]), w=[("xres", xsl)],
                  dma="xr%d" % xsl)
                banks = [0, 1, 2, 3] if tb % 2 == 0 else [4, 5, 6, 7]
                for cg in range(4):
                    for kc in range(KC):
                        A("pe", _mm(PS[banks[cg]][:, :], ycat[ysl][:, kc, within * 128:(within + 1) * 128],
                                    woutb[:, kc, cg * 512:(cg + 1) * 512], kc == 0, kc == KC - 1),
                          r=[("ycat", ysl, kc // 8), "woutb"], w=[("ps", banks[cg])])
                for cg in range(4):
                    A("act", _act(junk[:, cg * 512:(cg + 1) * 512], PS[banks[cg]][:, :], AF.Square),
                      r=[("ps", banks[cg])], w=["junk"])
                A("dve", lambda e: e.reduce_sum(out=ss1[:], in_=junk[:], axis=AX.X), r=["junk"], w=["ss1"])
                A("act", _act(ss1[:], ss1[:], AF.Ln, scale=1.0 / DM, bias=RMS_EPS), r=["ss1"], w=["ss1"])
                A("act", _act(rs1[:], ss1[:], AF.Exp, scale=-0.5), r=["ss1"], w=["rs1"])
                osl = tb % 2
                for cg in range(4):
                    cs = slice(cg * 512, (cg + 1) * 512)
                    A("dve", _stt(ot[osl][:, cs], PS[banks[cg]][:, :], rs1[:, 0:1], gpost_sb[:, cs],
                                  ALU.mult, ALU.mult),
                      r=[("ps", banks[cg]), "rs1", "gpost"], w=[("ot", osl, cg)])
                    A("pool", _tt(ot[osl][:, cs], ot[osl][:, cs], xres[xsl][:, cs], ALU.add),
                      r=[("ot", osl, cg), ("xres", xsl)], w=[("ot", osl, cg)])
                A("sp", _dma(out[tb * 128:(tb + 1) * 128, :], ot[osl][:]),
                  r=[("ot", osl, cg) for cg in range(4)], w=["out"], dma="o%d" % osl)
            sch.flush()
            esC.close()
    return nc


def _host_consts():
    p = np.arange(128)[:, None]
    c = np.arange(128)[None, :]
    m = np.zeros((128, NCONST, 128), np.float32)
    m[:, C_ONES] = 1.0
    m[:, C_NEGONES] = -1.0
    m[:, C_NEGTRI] = -(p >= c).astype(np.float32)
    m[:, C_IDENT] = (p == c).astype(np.float32)
    m[:, C_NEGBIG] = NEGBIG * (p == c).astype(np.float32)
    m[:, C_INVMASK] = (p >= c).astype(np.float32)
    return np.ascontiguousarray(m.reshape(128, NCONST * 128))


def make_in_maps(x, g_pre, w_in, conv_w, conv_b, ln_g, ln_b, w_pw2, b_pw2, w_out, g_post):
    f = lambda a: np.ascontiguousarray(np.asarray(a, dtype=np.float32))
    x = f(x)[0]
    S = x.shape[0]
    TOK = S // NCORES
    xT = np.ascontiguousarray(x.T)
    w_in = f(w_in)
    AW = 1024
    vec8 = lambda v: f(v).reshape(8, 128).T
    cvec = np.ascontiguousarray(np.stack([vec8(conv_b), vec8(ln_g), vec8(ln_b), vec8(b_pw2)], axis=1)
                                .reshape(128, 32))
    convw = np.ascontiguousarray(f(conv_w)[:, 0, :].T.reshape(8, 128, CK).transpose(1, 0, 2)
                                 .reshape(128, 8 * CK))
    gpre = np.ascontiguousarray(f(g_pre).reshape(KC, 128).T)
    gpost = np.ascontiguousarray(np.broadcast_to(f(g_post)[None, :], (128, DM)))
    wc = np.stack([w_in[:, 4 * AW:5 * AW], w_in[:, 5 * AW:6 * AW], w_in[:, 6 * AW:7 * AW]], axis=0)
    wc = wc.reshape(3, KC, 128, 8, 128).transpose(3, 0, 2, 1, 4)
    wconv = np.ascontiguousarray(wc.reshape(8 * 3 * 128, KC * 128))
    cmat = _host_consts()
    maps = []
    for c in range(NCORES):
        T0 = c * TOK
        xTown = np.zeros((DM, 32 + TOK), np.float32)
        xTown[:, 32:] = xT[:, T0:T0 + TOK]
        if c > 0:
            xTown[:, :32] = xT[:, T0 - 32:T0]
        whc = np.concatenate([w_in[:, k * AW + c * 128:k * AW + (c + 1) * 128] for k in range(4)], axis=1)
        maps.append({
            "xT": xT, "xTown": xTown, "xown": np.ascontiguousarray(x[T0:T0 + TOK]),
            "wh": np.ascontiguousarray(whc), "wconv": wconv, "wpw2": f(w_pw2), "wout": f(w_out),
            "gpre": gpre, "convw": convw, "cvec": cvec, "gpost": gpost, "cmat": cmat,
            "t0": np.array([[T0]], np.int32),
        })
    return maps


_CACHE = {}


AB_KEYS = ("cvec", "cmat", "xT", "xTown", "wh", "wconv", "wpw2", "gpre", "convw")
C_KEYS = ("cvec", "cmat", "xown", "wout", "gpost")


def kernel(x, g_pre, w_in, conv_w, conv_b, ln_g, ln_b, w_pw2, b_pw2, w_out, g_post):
    S = int(np.asarray(x).shape[1])
    TOK = S // NCORES
    maps = make_in_maps(x, g_pre, w_in, conv_w, conv_b, ln_g, ln_b, w_pw2, b_pw2, w_out, g_post)
    cores = list(range(NCORES))
    nc1 = build_program(S, "ab")
    r1 = run_bass_kernel_spmd(nc1, [{k: m[k] for k in AB_KEYS} for m in maps], core_ids=cores)
    ya = [np.asarray(r["yatt_loc"]) for r in r1.results]
    maps2 = []
    for c in range(NCORES):
        m = {k: maps[c][k] for k in C_KEYS}
        yin = np.stack([ya[h][:, c * TOK:(c + 1) * TOK] for h in range(NCORES)], axis=1)
        m["yatt_in"] = np.ascontiguousarray(yin.reshape(128, 8 * TOK))
        m["yconv_d"] = np.ascontiguousarray(np.asarray(r1.results[c]["yconv_d"]))
        maps2.append(m)
    nc2 = build_program(S, "c")
    r2 = run_bass_kernel_spmd(nc2, maps2, core_ids=cores)
    outs = [np.asarray(r["out"], dtype=np.float32) for r in r2.results]
    return np.concatenate(outs, axis=0)[None, :, :]
```
